# Optimizing a Trainium2 kernel written in Bass

```python
import math
import jax
import jax.numpy as jnp
from jax import lax
import numpy as np

D_MODEL = 1024
BATCH = 2
SEQ = 16384
DEPTH = 4

N_A_LAYERS = DEPTH // 2
N_B_LAYERS = DEPTH - N_A_LAYERS

GDN_HEADS = 8
GDN_DK = 128
GDN_DV = 256
GDN_KEY_W = GDN_HEADS * GDN_DK
GDN_VAL_W = GDN_HEADS * GDN_DV
GDN_CONV_W = 2 * GDN_KEY_W + GDN_VAL_W
Z_OFF = GDN_CONV_W
A_OFF = Z_OFF + GDN_VAL_W
B_OFF = A_OFF + GDN_HEADS
GDN_IN_W = B_OFF + GDN_HEADS
CONV_K = 4
CHUNK = 64

SB_HEADS = 4
SB_DH = 256
SB_W = SB_HEADS * SB_DH
SB_BLOCK = 128

EPS = 1e-6

kernel_name = 'yoco_gdn_stickbreaking_trunk'


def rmsnorm(x, g):
    xf = x.astype(jnp.float32)
    y = xf * lax.rsqrt(jnp.mean(xf * xf, axis=-1, keepdims=True) + EPS)
    return (y * g.astype(jnp.float32)).astype(x.dtype)


def l2norm(x):
    return x * lax.rsqrt(jnp.sum(x * x, axis=-1, keepdims=True) + EPS)


def causal_depthwise_conv(x, w):
    k_size, ch = w.shape
    return lax.conv_general_dilated(
        x, w[:, None, :].astype(x.dtype), window_strides=(1,),
        padding=[(k_size - 1, 0)], dimension_numbers=('NWC', 'WIO', 'NWC'),
        feature_group_count=ch)


def to_chunks(t):
    b, s, h = t.shape[:3]
    t = t.reshape((b, s // CHUNK, CHUNK, h) + t.shape[3:])
    return jnp.moveaxis(t, 3, 1)


def gated_delta_rule(q, k, v, g, beta):
    b, s, h, dk = q.shape
    dv = v.shape[-1]
    q, k, v, g, beta = (to_chunks(t) for t in (q, k, v, g, beta))
    gc = jnp.cumsum(g, axis=-1)
    idx = jnp.arange(CHUNK)
    incl = idx[:, None] >= idx[None, :]
    strict = idx[:, None] > idx[None, :]
    diff = gc[..., :, None] - gc[..., None, :]
    decay = jnp.where(incl, jnp.exp(jnp.where(incl, diff, 0.0)), 0.0)
    kb = k * beta[..., None]
    a_mat = jnp.where(strict, jnp.einsum('bhnid,bhnjd->bhnij', kb, k) * decay, 0.0)
    eye = jnp.eye(CHUNK, dtype=a_mat.dtype)
    rhs = jnp.concatenate([v * beta[..., None], kb * jnp.exp(gc)[..., None]], axis=-1)
    sol = lax.linalg.triangular_solve(a_mat + eye, rhs, left_side=True, lower=True,
                                      unit_diagonal=True)
    w_val, k_cum = sol[..., :dv], sol[..., dv:]
    qk = jnp.einsum('bhnid,bhnjd->bhnij', q, k) * decay
    q_dec = q * jnp.exp(gc)[..., None]
    k_dec = k * jnp.exp(gc[..., -1:] - gc)[..., None]
    g_last = jnp.exp(gc[..., -1])

    def step(state, inp):
        qk_n, qd_n, kd_n, wv_n, kc_n, gl_n = inp
        v_new = wv_n - jnp.einsum('bhik,bhkv->bhiv', kc_n, state)
        o_n = (jnp.einsum('bhik,bhkv->bhiv', qd_n, state)
               + jnp.einsum('bhij,bhjv->bhiv', qk_n, v_new))
        state = state * gl_n[..., None, None] + jnp.einsum('bhik,bhiv->bhkv', kd_n, v_new)
        return state, o_n

    xs = tuple(jnp.moveaxis(t, 2, 0) for t in (qk, q_dec, k_dec, w_val, k_cum, g_last))
    state0 = jnp.zeros((b, h, dk, dv), jnp.float32)
    _, o = lax.scan(step, state0, xs)
    return jnp.transpose(o, (1, 0, 3, 2, 4)).reshape(b, s, h, dv)


def gdn_layer(x, norm_g, w_in, conv_w, a_log, dt_bias, out_g, w_out):
    b, s, _ = x.shape
    proj = rmsnorm(x, norm_g) @ w_in
    qkv = jax.nn.silu(causal_depthwise_conv(proj[..., :GDN_CONV_W], conv_w)).astype(jnp.float32)
    z = proj[..., Z_OFF:A_OFF].astype(jnp.float32)
    a_in = proj[..., A_OFF:B_OFF].astype(jnp.float32)
    b_in = proj[..., B_OFF:].astype(jnp.float32)
    q = l2norm(qkv[..., :GDN_KEY_W].reshape(b, s, GDN_HEADS, GDN_DK)) * (GDN_DK ** -0.5)
    k = l2norm(qkv[..., GDN_KEY_W:2 * GDN_KEY_W].reshape(b, s, GDN_HEADS, GDN_DK))
    v = qkv[..., 2 * GDN_KEY_W:].reshape(b, s, GDN_HEADS, GDN_DV)
    g = -jnp.exp(a_log.astype(jnp.float32)) * jax.nn.softplus(a_in + dt_bias.astype(jnp.float32))
    beta = jax.nn.sigmoid(b_in)
    o = gated_delta_rule(q, k, v, g, beta)
    o = rmsnorm(o, out_g) * jax.nn.silu(z.reshape(b, s, GDN_HEADS, GDN_DV))
    return x + o.reshape(b, s, GDN_VAL_W).astype(x.dtype) @ w_out


def shared_kv(x, kv_norm, w_kv):
    b, s, _ = x.shape
    kv = rmsnorm(x, kv_norm) @ w_kv
    k = kv[..., :SB_W].reshape(b, s, SB_HEADS, SB_DH).transpose(0, 2, 1, 3).astype(jnp.float32)
    v = kv[..., SB_W:].reshape(b, s, SB_HEADS, SB_DH).transpose(0, 2, 1, 3).astype(jnp.float32)
    return k, v


def stick_breaking_attention(q, k, v):
    b, h, s_len, _ = q.shape
    sub = jnp.arange(SB_BLOCK)
    upper = (sub[:, None] >= sub[None, :]).astype(jnp.float32)
    outs = []
    for i in range(s_len // SB_BLOCK):
        n_kb = i + 1
        kv_len = n_kb * SB_BLOCK
        qb = q[:, :, i * SB_BLOCK:kv_len]
        kb = k[:, :, :kv_len]
        vb = v[:, :, :kv_len]
        z = jnp.einsum('bhqd,bhkd->bhqk', qb, kb)
        t_idx = i * SB_BLOCK + sub[:, None]
        s_idx = jnp.arange(kv_len)[None, :]
        mask = s_idx < t_idx
        log1m = jnp.where(mask, -jax.nn.softplus(z), 0.0)
        log1m = log1m.reshape(b, h, SB_BLOCK, n_kb, SB_BLOCK)
        rc_in = jnp.einsum('bhqnj,js->bhqns', log1m, upper,
                           precision=lax.Precision.HIGHEST)
        totals = rc_in[..., 0]
        later = lax.cumsum(totals, axis=3, reverse=True) - totals
        rc = (rc_in + later[..., None]).reshape(b, h, SB_BLOCK, kv_len)
        attn = jnp.where(mask, jnp.exp(z + rc), 0.0)
        outs.append(jnp.einsum('bhqk,bhkd->bhqd', attn, vb))
    return jnp.concatenate(outs, axis=2)


def sb_layer(x, norm_g, w_in, w_out, k_sh, v_sh):
    b, s, _ = x.shape
    proj = rmsnorm(x, norm_g) @ w_in
    q = proj[..., :SB_W].reshape(b, s, SB_HEADS, SB_DH).transpose(0, 2, 1, 3).astype(jnp.float32)
    q = q * (SB_DH ** -0.5)
    z = proj[..., SB_W:]
    o = stick_breaking_attention(q, k_sh, v_sh)
    o = o.transpose(0, 2, 1, 3).reshape(b, s, SB_W).astype(x.dtype) * jax.nn.silu(z)
    return x + o @ w_out


def setup_inputs(seed: int = 0) -> dict:
    key = jax.random.key(seed)
    ks = jax.random.split(key, 14)
    f32 = jnp.float32
    nrm = jax.random.normal
    x = nrm(ks[0], (BATCH, SEQ, D_MODEL), f32)
    a_norm = 1.0 + 0.01 * nrm(ks[1], (N_A_LAYERS, D_MODEL), f32)
    a_w_in = nrm(ks[2], (N_A_LAYERS, D_MODEL, GDN_IN_W), f32) * D_MODEL ** -0.5
    a_conv = nrm(ks[3], (N_A_LAYERS, CONV_K, GDN_CONV_W), f32) * CONV_K ** -0.5
    a_A_log = jnp.log(jax.random.uniform(ks[4], (N_A_LAYERS, GDN_HEADS), f32, 1.0, 16.0))
    dt = jnp.exp(jax.random.uniform(ks[5], (N_A_LAYERS, GDN_HEADS), f32,
                                    math.log(1e-3), math.log(1e-1)))
    a_dt_bias = dt + jnp.log(-jnp.expm1(-dt))
    a_out_norm = 1.0 + 0.01 * nrm(ks[6], (N_A_LAYERS, GDN_DV), f32)
    a_w_out = nrm(ks[7], (N_A_LAYERS, GDN_VAL_W, D_MODEL), f32) * GDN_VAL_W ** -0.5
    kv_norm = 1.0 + 0.01 * nrm(ks[8], (D_MODEL,), f32)
    w_kv = nrm(ks[9], (D_MODEL, 2 * SB_W), f32) * D_MODEL ** -0.5
    b_norm = 1.0 + 0.01 * nrm(ks[10], (N_B_LAYERS, D_MODEL), f32)
    b_w_in = nrm(ks[11], (N_B_LAYERS, D_MODEL, 2 * SB_W), f32) * D_MODEL ** -0.5
    b_w_out = nrm(ks[12], (N_B_LAYERS, SB_W, D_MODEL), f32) * SB_W ** -0.5
    final_norm = 1.0 + 0.01 * nrm(ks[13], (D_MODEL,), f32)
    return {'x': x, 'a_norm': a_norm, 'a_w_in': a_w_in, 'a_conv': a_conv,
            'a_A_log': a_A_log, 'a_dt_bias': a_dt_bias, 'a_out_norm': a_out_norm,
            'a_w_out': a_w_out, 'kv_norm': kv_norm, 'w_kv': w_kv, 'b_norm': b_norm,
            'b_w_in': b_w_in, 'b_w_out': b_w_out, 'final_norm': final_norm}


def reference(x, a_norm, a_w_in, a_conv, a_A_log, a_dt_bias, a_out_norm, a_w_out,
              kv_norm, w_kv, b_norm, b_w_in, b_w_out, final_norm):
    k_sh = None
    v_sh = None
    for layer in range(DEPTH):
        if layer < N_A_LAYERS:
            x = gdn_layer(x, a_norm[layer], a_w_in[layer], a_conv[layer], a_A_log[layer],
                          a_dt_bias[layer], a_out_norm[layer], a_w_out[layer])
        else:
            if layer == N_A_LAYERS:
                k_sh, v_sh = shared_kv(x, kv_norm, w_kv)
            j = layer - N_A_LAYERS
            x = sb_layer(x, b_norm[j], b_w_in[j], b_w_out[j], k_sh, v_sh)
    return rmsnorm(x, final_norm)
```

```python
import contextlib
import numpy as np
import ml_dtypes
import concourse.bass as bass
import concourse.mybir as mybir
from concourse.bass_utils import run_bass_kernel_spmd

F32 = mybir.dt.float32
BF16 = mybir.dt.bfloat16
AF = mybir.ActivationFunctionType
ALU = mybir.AluOpType

EPS = 1e-6
D = 1024
NKC = 8
SEM_LIMIT = 30000


class Buf:
    __slots__ = ("name", "w", "r", "dsem", "dcount", "psum")

    def __init__(self, name, psum=False):
        self.name = name
        self.w = None
        self.r = {}
        self.dsem = None
        self.dcount = 0
        self.psum = psum


class EngW:
    def __init__(self, fw, name, eng, is_pe=False):
        self.fw = fw
        self.name = name
        self.eng = eng
        self.is_pe = is_pe
        self.sem = None
        self.count = 0
        self.seen = {}
        self.last = None

    def _rotate(self):
        self.sem = self.fw.new_sem(self.name)
        self.count = 0

    def wait(self, tok):
        if tok is None:
            return
        sem, val, who = tok
        if who is self and self.is_pe:
            return
        key = id(sem)
        if self.seen.get(key, (None, 0))[1] >= val:
            return
        self.eng.wait_ge(sem, val)
        self.seen[key] = (sem, val)

    def issued(self, instr):
        if self.sem is None or self.count >= SEM_LIMIT:
            self._rotate()
        self.count += 1
        instr.then_inc(self.sem, 1)
        self.last = (self.sem, self.count, self)
        return self.last


class FW:
    def __init__(self, nc):
        self.nc = nc
        self.es = contextlib.ExitStack()
        self.nsem = 0
        self.eng = {
            "pe": EngW(self, "pe", nc.tensor, is_pe=True),
            "act": EngW(self, "act", nc.scalar),
            "dve": EngW(self, "dve", nc.vector),
            "pool": EngW(self, "pool", nc.gpsimd),
            "sp": EngW(self, "sp", nc.sync),
        }
        self.dma_toks = {}

    def new_sem(self, name):
        self.nsem += 1
        return self.es.enter_context(self.nc.semaphore(f"s{self.nsem}_{name}"))

    def _deps(self, e, reads, writes):
        for b in reads:
            e.wait(b.w)
            if b.psum:
                for t in b.r.values():
                    if t[2] is not e:
                        e.wait(t)
        for b in writes:
            e.wait(b.w)
            for t in b.r.values():
                e.wait(t)

    def _update(self, tok, reads, writes):
        key = id(tok[0])
        for b in reads:
            b.r[key] = tok
        for b in writes:
            b.w = tok
            b.r = {}

    def op(self, en, fn, reads=(), writes=()):
        e = self.eng[en]
        self._deps(e, reads, writes)
        tok = e.issued(fn(e.eng))
        self._update(tok, reads, writes)
        return tok

    def dma(self, en, out, in_, reads=(), writes=(), owner=None):
        e = self.eng[en]
        self._deps(e, reads, writes)
        if owner is None:
            owner = writes[0] if writes else reads[0]
        if owner.dsem is None:
            owner.dsem = self.new_sem("d_" + owner.name)
        owner.dcount += 16
        e.eng.dma_start(out=out, in_=in_).then_inc(owner.dsem, 16)
        tok = (owner.dsem, owner.dcount, None)
        self.dma_toks[id(owner.dsem)] = tok
        self._update(tok, reads, writes)
        return tok

    def barrier(self):
        lasts = [e.last for e in self.eng.values() if e.last is not None]
        for e in self.eng.values():
            for t in lasts:
                if t[2] is not e:
                    e.wait(t)
            for t in self.dma_toks.values():
                e.wait(t)

    def close(self):
        self.es.close()


def ring(name, n, psum=False):
    return [Buf(f"{name}{i}", psum) for i in range(n)]


def load_cast_bf16(fw, es, dst, dst_buf, src_aps, stage, stage_bufs, width):
    for i, src in enumerate(src_aps):
        sb = stage_bufs[i % len(stage_bufs)]
        st = stage[i % len(stage)]
        fw.dma("sp", st[:, :width], src, writes=[sb])
        fw.op("dve", lambda v, i=i, st=st: v.tensor_copy(out=dst[:, i, :width], in_=st[:, :width]),
              reads=[sb], writes=[dst_buf])


def prologue_phase(fw, S, prev_kc, xT_in, oT_prev, w_out, xT_out, TT=512, final_g=None):
    nc = fw.nc
    TT = min(TT, S)
    with contextlib.ExitStack() as es:
        wsb = es.enter_context(nc.sbuf_tensor("pl_w", [128, prev_kc, D], BF16))
        stage = [es.enter_context(nc.sbuf_tensor(f"pl_st{i}", [128, D], F32)) for i in range(2)]
        xt = [es.enter_context(nc.sbuf_tensor(f"pl_x{i}", [128, NKC, TT], F32)) for i in range(2)]
        ot = [es.enter_context(nc.sbuf_tensor(f"pl_o{i}", [128, prev_kc, TT], BF16)) for i in range(2)]
        ps = [es.enter_context(nc.psum_tensor(f"pl_ps{i}", [128, TT], F32)) for i in range(4)]
        b_w = Buf("pl_w")
        b_st = ring("pl_st", 2)
        b_x = ring("pl_x", 2)
        b_o = ring("pl_o", 2)
        b_ps = ring("pl_ps", 4, True)
        load_cast_bf16(fw, es, wsb, b_w, [w_out[kc] for kc in range(prev_kc)], stage, b_st, D)
        if final_g is not None:
            sq = es.enter_context(nc.sbuf_tensor("pl_sq", [128, NKC, TT], BF16))
            rstd = es.enter_context(nc.sbuf_tensor("pl_rstd", [128, TT], F32))
            ones_bf = es.enter_context(nc.sbuf_tensor("pl_ones", [128, 128], BF16))
            gfin = es.enter_context(nc.sbuf_tensor("pl_gfin", [128, NKC], F32))
            b_sq, b_rstd = Buf("pl_sq"), Buf("pl_rstd")
            fw.dma("sp", gfin[:], final_g, writes=[b_w])
            fw.op("dve", lambda v: v.memset(ones_bf[:], 1.0), writes=[b_w])
        nt = S // TT
        n = 0
        for it in range(nt):
            t0 = it * TT
            X, bX = xt[it % 2], b_x[it % 2]
            O, bO = ot[it % 2], b_o[it % 2]
            fw.dma("sp", X[:], xT_in[:, :, t0:t0 + TT].rearrange("k p t -> p k t"), writes=[bX])
            fw.dma("sp", O[:], oT_prev[:, :, t0:t0 + TT].rearrange("k p t -> p k t"), writes=[bO])
            for dc in range(NKC):
                P, bP = ps[n % 4], b_ps[n % 4]
                n += 1
                for kc in range(prev_kc):
                    fw.op("pe", lambda pe, kc=kc, dc=dc, P=P, O=O: pe.matmul(
                        P[:], lhsT=wsb[:, kc, dc * 128:(dc + 1) * 128], rhs=O[:, kc, :],
                        start=(kc == 0), stop=(kc == prev_kc - 1)),
                        reads=[b_w, bO], writes=[bP])
                fw.op("dve", lambda v, dc=dc, P=P, X=X: v.tensor_tensor(
                    out=X[:, dc, :], in0=X[:, dc, :], in1=P[:], op=ALU.add), reads=[bP, bX], writes=[bX])
            if final_g is not None:
                P, bP = ps[n % 4], b_ps[n % 4]
                n += 1
                rms_frontend(fw, X, bX, TT, sq, b_sq, ones_bf, b_w, P, bP, rstd, b_rstd)
                for kc in range(NKC):
                    fw.op("dve", lambda v, kc=kc, X=X: v.scalar_tensor_tensor(
                        out=X[:, kc, :], in0=X[:, kc, :], scalar=gfin[:, kc:kc + 1], in1=rstd[:, :TT],
                        op0=ALU.mult, op1=ALU.mult), reads=[bX, b_rstd, b_w], writes=[bX])
            fw.dma("pool", xT_out[:, :, t0:t0 + TT].rearrange("k p t -> p k t"), X[:], reads=[bX])
        fw.barrier()


def rms_frontend(fw, X, bX, T, sq, b_sq, ones_bf, b_c, P, bP, rstd, b_rstd):
    for kc in range(NKC):
        fw.op("act", lambda a, kc=kc: a.activation(out=sq[:, kc, :T], in_=X[:, kc, :T], func=AF.Square),
              reads=[bX], writes=[b_sq])
    for kc in range(NKC):
        fw.op("pe", lambda pe, kc=kc: pe.matmul(P[:, :T], lhsT=ones_bf[:], rhs=sq[:, kc, :T],
                                               start=(kc == 0), stop=(kc == NKC - 1)),
              reads=[b_sq, b_c], writes=[bP])
    fw.op("dve", lambda v: v.tensor_scalar(out=rstd[:, :T], in0=P[:, :T], scalar1=1.0 / D, scalar2=EPS,
                                          op0=ALU.mult, op1=ALU.add), reads=[bP], writes=[b_rstd])
    fw.op("act", lambda a: a.activation(out=rstd[:, :T], in_=rstd[:, :T], func=AF.Ln), reads=[b_rstd], writes=[b_rstd])
    fw.op("act", lambda a: a.activation(out=rstd[:, :T], in_=rstd[:, :T], func=AF.Exp, scale=-0.5),
          reads=[b_rstd], writes=[b_rstd])


def sb_layer_phase(fw, S, first, xT, wq_d, wz_d, wk_d, wv_d, gb_d, gkv_d, ctri_d, cmask_d,
                   KT_d, V_d, oT_out, QT=512, FT=256):
    nc = fw.nc
    NB = S // 128
    with contextlib.ExitStack() as es:
        def sb(name, shape, dt):
            return es.enter_context(nc.sbuf_tensor(name, shape, dt))
        KT = sb("sb_KT", [128, 2, S], BF16)
        V = sb("sb_V", [128, NB, 256], BF16)
        wq = sb("sb_wq", [128, NKC, 256], BF16)
        wz = sb("sb_wz", [128, NKC, 256], BF16)
        if first:
            wk = sb("sb_wk", [128, NKC, 256], BF16)
            wv = sb("sb_wv", [128, NKC, 256], BF16)
            gkv = sb("sb_gkv", [128, NKC], F32)
        gb = sb("sb_gb", [128, NKC], F32)
        stage = [sb(f"sb_st{i}", [128, 256], F32) for i in range(2)]
        tri = sb("sb_tri", [128, 2, 128], BF16)
        ones_bf = sb("sb_ones", [128, 128], BF16)
        mask = sb("sb_mask", [128, 8, 256], BF16)
        xt = [sb(f"sb_x{i}", [128, NKC, FT], F32) for i in range(2)]
        sq = sb("sb_sq", [128, NKC, FT], BF16)
        rstd = sb("sb_rstd", [128, FT], F32)
        xnb = sb("sb_xnb", [128, NKC, FT], BF16)
        if first:
            xnk = sb("sb_xnk", [128, NKC, FT], BF16)
        qT = sb("sb_qT", [128, 2, QT], BF16)
        zs = sb("sb_zs", [128, 2, QT], F32)
        e_t = [sb(f"sb_e{i}", [128, QT], F32) for i in range(2)]
        L_t = [sb(f"sb_L{i}", [128, QT], BF16) for i in range(4)]
        xr_t = [sb(f"sb_xr{i}", [128, QT], F32) for i in range(2)]
        at_t = [sb(f"sb_at{i}", [128, QT], BF16) for i in range(3)]
        ost = [sb(f"sb_ost{i}", [128, 2, QT], BF16) for i in range(2)]
        ps = [es.enter_context(nc.psum_tensor(f"sb_ps{i}", [128, 512], F32)) for i in range(8)]

        b_KT = [Buf(f"KT{i}") for i in range(S // QT)]
        b_V = [Buf(f"V{i}") for i in range(S // QT)]
        b_c = Buf("consts")
        b_st = ring("st", 2)
        b_x = ring("x", 2)
        b_sq, b_rstd, b_xnb, b_xnk, b_qT, b_zs = (Buf(n) for n in ("sq", "rstd", "xnb", "xnk", "qT", "zs"))
        b_e, b_L, b_xr, b_at, b_ost = ring("e", 2), ring("L", 4), ring("xr", 2), ring("at", 3), ring("ost", 2)
        b_ps = ring("ps", 8, True)
        Z, bZ = ps[0:2], b_ps[0:2]
        Rp, bR = ps[2], b_ps[2]
        OT, bOT = ps[3:5], b_ps[3:5]
        FP, bFP = ps[5:8], b_ps[5:8]

        load_cast_bf16(fw, es, wq, b_c, [wq_d[kc] for kc in range(NKC)], stage, b_st, 256)
        load_cast_bf16(fw, es, wz, b_c, [wz_d[kc] for kc in range(NKC)], stage, b_st, 256)
        if first:
            load_cast_bf16(fw, es, wk, b_c, [wk_d[kc] for kc in range(NKC)], stage, b_st, 256)
            load_cast_bf16(fw, es, wv, b_c, [wv_d[kc] for kc in range(NKC)], stage, b_st, 256)
            fw.dma("sp", gkv[:], gkv_d, writes=[b_c])
        load_cast_bf16(fw, es, tri, b_c, [ctri_d[i] for i in range(2)], stage, b_st, 128)
        fw.dma("sp", gb[:], gb_d, writes=[b_c])
        load_cast_bf16(fw, es, mask, b_c, [cmask_d[i // 2][:, (i % 2) * 256:(i % 2 + 1) * 256] for i in range(8)], stage, b_st, 256)
        fw.op("dve", lambda v: v.memset(ones_bf[:], 1.0), writes=[b_c])
        if not first:
            for i in range(S // QT):
                fw.dma("sp", KT[:, :, i * QT:(i + 1) * QT],
                       KT_d[:, :, i * QT:(i + 1) * QT].rearrange("c p t -> p c t"), writes=[b_KT[i]])
                nb = QT // 128
                fw.dma("sp", V[:, i * nb:(i + 1) * nb, :],
                       V_d[i * nb:(i + 1) * nb].rearrange("n p d -> p n d"), writes=[b_V[i]])

        nfp = [0]

        def fp_bank():
            k = nfp[0] % 3
            nfp[0] += 1
            return FP[k], bFP[k]

        nfe = 0
        gstep = [0]
        for qi in range(S // QT):
            q0 = qi * QT
            for hf in range(QT // FT):
                t0 = q0 + hf * FT
                X, bX = xt[nfe % 2], b_x[nfe % 2]
                nfe += 1
                fw.dma("sp", X[:], xT[:, :, t0:t0 + FT].rearrange("k p t -> p k t"), writes=[bX])
                P, bP = fp_bank()
                rms_frontend(fw, X, bX, FT, sq, b_sq, ones_bf, b_c, P, bP, rstd, b_rstd)
                for kc in range(NKC):
                    fw.op("dve", lambda v, kc=kc: v.scalar_tensor_tensor(
                        out=xnb[:, kc, :], in0=X[:, kc, :], scalar=gb[:, kc:kc + 1], in1=rstd[:, :FT],
                        op0=ALU.mult, op1=ALU.mult), reads=[bX, b_rstd, b_c], writes=[b_xnb])
                    if first:
                        fw.op("dve", lambda v, kc=kc: v.scalar_tensor_tensor(
                            out=xnk[:, kc, :], in0=X[:, kc, :], scalar=gkv[:, kc:kc + 1], in1=rstd[:, :FT],
                            op0=ALU.mult, op1=ALU.mult), reads=[bX, b_rstd, b_c], writes=[b_xnk])
                cs = slice(hf * FT, (hf + 1) * FT)
                for c in range(2):
                    P, bP = fp_bank()
                    for kc in range(NKC):
                        fw.op("pe", lambda pe, kc=kc, c=c, P=P: pe.matmul(
                            P[:, :FT], lhsT=wq[:, kc, c * 128:(c + 1) * 128], rhs=xnb[:, kc, :],
                            start=(kc == 0), stop=(kc == NKC - 1)), reads=[b_c, b_xnb], writes=[bP])
                    fw.op("act", lambda a, c=c, P=P: a.activation(out=qT[:, c, cs], in_=P[:, :FT], func=AF.Copy,
                                                                 scale=1.0 / 16.0), reads=[bP], writes=[b_qT])
                    P, bP = fp_bank()
                    for kc in range(NKC):
                        fw.op("pe", lambda pe, kc=kc, c=c, P=P: pe.matmul(
                            P[:, :FT], lhsT=wz[:, kc, c * 128:(c + 1) * 128], rhs=xnb[:, kc, :],
                            start=(kc == 0), stop=(kc == NKC - 1)), reads=[b_c, b_xnb], writes=[bP])
                    fw.op("act", lambda a, c=c, P=P: a.activation(out=zs[:, c, cs], in_=P[:, :FT], func=AF.Silu),
                          reads=[bP], writes=[b_zs])
                    if first:
                        P, bP = fp_bank()
                        for kc in range(NKC):
                            fw.op("pe", lambda pe, kc=kc, c=c, P=P: pe.matmul(
                                P[:, :FT], lhsT=wk[:, kc, c * 128:(c + 1) * 128], rhs=xnk[:, kc, :],
                                start=(kc == 0), stop=(kc == NKC - 1)), reads=[b_c, b_xnk], writes=[bP])
                        fw.op("act", lambda a, c=c, P=P: a.activation(out=KT[:, c, t0:t0 + FT], in_=P[:, :FT],
                                                                     func=AF.Copy), reads=[bP], writes=[b_KT[qi]])
                if first:
                    for tb in range(FT // 128):
                        P, bP = fp_bank()
                        for kc in range(NKC):
                            fw.op("pe", lambda pe, kc=kc, tb=tb, P=P: pe.matmul(
                                P[:, :256], lhsT=xnk[:, kc, tb * 128:(tb + 1) * 128], rhs=wv[:, kc, :],
                                start=(kc == 0), stop=(kc == NKC - 1)), reads=[b_c, b_xnk], writes=[bP])
                        blk = t0 // 128 + tb
                        fw.op("act", lambda a, blk=blk, P=P: a.activation(out=V[:, blk, :], in_=P[:, :256],
                                                                         func=AF.Copy), reads=[bP], writes=[b_V[qi]])
            blocks = list(range(4 * qi + 3, -1, -1))
            n = len(blocks)

            def stageA(s):
                j = blocks[s]
                g = gstep[0] + s
                Zp, bZp = Z[g % 2], bZ[g % 2]
                E, bE = e_t[g % 2], b_e[g % 2]
                Lx, bL = L_t[g % 4], b_L[g % 4]
                kq = j * 128 // QT
                for c in range(2):
                    fw.op("pe", lambda pe, c=c: pe.matmul(Zp[:], lhsT=KT[:, c, j * 128:(j + 1) * 128], rhs=qT[:, c, :],
                                                         start=(c == 0), stop=(c == 1)),
                          reads=[b_KT[kq], b_qT], writes=[bZp])
                fw.op("act", lambda a: a.activation(out=E[:], in_=Zp[:], func=AF.Exp), reads=[bZp], writes=[bE])
                m = j - 4 * qi
                if m >= 0:
                    fw.op("dve", lambda v: v.tensor_tensor(out=E[:], in0=E[:], in1=mask[:, 2 * m:2 * m + 2, :].rearrange("p a b -> p (a b)"), op=ALU.mult),
                          reads=[bE, b_c], writes=[bE])
                fw.op("act", lambda a: a.activation(out=Lx[:], in_=E[:], func=AF.Ln, bias=1.0),
                      reads=[bE], writes=[bL])

            def stageB(s):
                g = gstep[0] + s
                E, bE = e_t[g % 2], b_e[g % 2]
                Lx, bL = L_t[g % 4], b_L[g % 4]
                XR, bXR = xr_t[g % 2], b_xr[g % 2]
                AT, bAT = at_t[g % 3], b_at[g % 3]
                if s > 0:
                    Lp, bLp = L_t[(g - 1) % 4], b_L[(g - 1) % 4]
                    fw.op("pe", lambda pe: pe.matmul(Rp[:], lhsT=tri[:, 1, :], rhs=Lp[:], start=False, stop=False,
                                                    skip_group_check=True), reads=[b_c, bLp], writes=[bR])
                fw.op("pe", lambda pe: pe.matmul(Rp[:], lhsT=tri[:, 0, :], rhs=Lx[:], start=(s == 0), stop=True,
                                                skip_group_check=True), reads=[b_c, bL], writes=[bR])
                fw.op("act", lambda a: a.activation(out=XR[:], in_=Rp[:], func=AF.Exp), reads=[bR], writes=[bXR])
                fw.op("dve", lambda v: v.tensor_tensor(out=AT[:], in0=E[:], in1=XR[:], op=ALU.mult),
                      reads=[bE, bXR], writes=[bAT])

            def stageC(s):
                j = blocks[s]
                g = gstep[0] + s
                AT, bAT = at_t[g % 3], b_at[g % 3]
                kq = j * 128 // QT
                for c in range(2):
                    fw.op("pe", lambda pe, c=c: pe.matmul(OT[c][:], lhsT=V[:, j, c * 128:(c + 1) * 128], rhs=AT[:],
                                                         start=(s == 0), stop=(s == n - 1), skip_group_check=True),
                          reads=[b_V[kq], bAT], writes=[bOT[c]])

            for s in range(n + 2):
                if s < n:
                    stageA(s)
                if 1 <= s <= n:
                    stageB(s - 1)
                if 2 <= s <= n + 1:
                    stageC(s - 2)
            gstep[0] += n
            OS, bOS = ost[qi % 2], b_ost[qi % 2]
            for c in range(2):
                fw.op("dve", lambda v, c=c: v.tensor_tensor(out=OS[:, c, :], in0=OT[c][:], in1=zs[:, c, :], op=ALU.mult),
                      reads=[bOT[c], b_zs], writes=[bOS])
            fw.dma("pool", oT_out[:, :, q0:q0 + QT].rearrange("c p t -> p c t"), OS[:], reads=[bOS])
        if first:
            for i in range(S // QT):
                fw.dma("pool", KT_d[:, :, i * QT:(i + 1) * QT].rearrange("c p t -> p c t"),
                       KT[:, :, i * QT:(i + 1) * QT], reads=[b_KT[i]])
                nb = QT // 128
                fw.dma("pool", V_d[i * nb:(i + 1) * nb].rearrange("n p d -> p n d"),
                       V[:, i * nb:(i + 1) * nb, :], reads=[b_V[i]])
        fw.barrier()


def dram_in(nc, name, shape, dt=F32):
    return nc.dram_tensor(name, list(shape), dt, kind="ExternalInput").ap()


def dram_out(nc, name, shape, dt=F32):
    return nc.dram_tensor(name, list(shape), dt, kind="ExternalOutput").ap()


def build_sb_program(S, first, prev_kc):
    nc = bass.Bass("TRN2", target_bir_lowering=False)
    fw = FW(nc)
    xT_in = dram_in(nc, "xT_in", [NKC, 128, S])
    oT_prev = dram_in(nc, "oT_prev", [prev_kc, 128, S], BF16)
    w_out = dram_in(nc, "w_out", [prev_kc, 128, D])
    wq = dram_in(nc, "wq", [NKC, 128, 256])
    wz = dram_in(nc, "wz", [NKC, 128, 256])
    gb = dram_in(nc, "gb", [128, NKC])
    ctri = dram_in(nc, "ctri", [2, 128, 128])
    cmask = dram_in(nc, "cmask", [4, 128, 512])
    wk = wv = gkv = None
    if first:
        wk = dram_in(nc, "wk", [NKC, 128, 256])
        wv = dram_in(nc, "wv", [NKC, 128, 256])
        gkv = dram_in(nc, "gkv", [128, NKC])
        KT_d = dram_out(nc, "KT", [2, 128, S], BF16)
        V_d = dram_out(nc, "V", [S // 128, 128, 256], BF16)
    else:
        KT_d = dram_in(nc, "KT", [2, 128, S], BF16)
        V_d = dram_in(nc, "V", [S // 128, 128, 256], BF16)
    xT_out = dram_out(nc, "xT_out", [NKC, 128, S])
    oT_out = dram_out(nc, "oT_out", [2, 128, S], BF16)
    prologue_phase(fw, S, prev_kc, xT_in, oT_prev, w_out, xT_out)
    sb_layer_phase(fw, S, first, xT_out, wq, wz, wk, wv, gb, gkv, ctri, cmask, KT_d, V_d, oT_out)
    fw.close()
    return nc


def sb_consts():
    k = np.arange(128)
    ctri = np.zeros((2, 128, 128), np.float32)
    ctri[0] = -(k[:, None] >= k[None, :]).astype(np.float32)
    ctri[1] = -(k[:, None] < k[None, :]).astype(np.float32)
    n = np.arange(512)
    cmask = np.zeros((4, 128, 512), np.float32)
    for m in range(4):
        cmask[m] = ((128 * m + k)[:, None] < n[None, :]).astype(np.float32)
    return ctri, cmask


def fm(a):
    T, Fd = a.shape
    return np.ascontiguousarray(a.T.reshape(Fd // 128, 128, T))


def rows(w):
    K, N = w.shape
    return np.ascontiguousarray(w.reshape(K // 128, 128, N))


def gain_pk(g):
    return np.ascontiguousarray(g.reshape(-1, 128).T)


GDN_STOP = [99]
GDN_DBG = [None]
GDN_BAR = [set()]
CM_MLE, CM_MGT, CM_ONESBD, CM_I, CM_NEGS, CM_NEGI, CM_SEL0, CM_SEL1, CM_ONES, CM_NEGID, CM_NEGST, CM_MK8, CM_MK16, CM_MK32, CM_MK64 = range(15)
NEG_BIG = -30000.0


def gdn_consts():
    t = np.arange(128)
    same = (t[:, None] // 64) == (t[None, :] // 64)
    tl = t % 64
    cm = np.zeros((15, 128, 128), np.float32)
    cm[CM_MLE] = same & (tl[:, None] <= tl[None, :])
    cm[CM_MGT] = same & (tl[:, None] > tl[None, :])
    cm[CM_ONESBD] = same
    cm[CM_I] = np.eye(128)
    cm[CM_NEGS] = np.where(same & (tl[None, :] > tl[:, None]), 0.0, NEG_BIG)
    cm[CM_NEGI] = np.where(same & (tl[None, :] >= tl[:, None]), 0.0, NEG_BIG)
    cm[CM_SEL0] = (t[:, None] < 64) * np.ones((1, 128))
    cm[CM_SEL1] = (t[:, None] >= 64) * np.ones((1, 128))
    cm[CM_ONES] = 1.0
    cm[CM_NEGID] = -np.eye(128)
    cm[CM_NEGST] = cm[CM_NEGS].T
    blk = lambda n: (t[:, None] // n) == (t[None, :] // n)
    cm[CM_MK8] = blk(8)
    cm[CM_MK16] = blk(16) & ~blk(8)
    cm[CM_MK32] = blk(32) & ~blk(16)
    cm[CM_MK64] = blk(64) & ~blk(32)
    return cm


def gdn_layer_phase(fw, S, xT, w12_d, wab_d, cw_d, gb_d, hs_d, og_d, cm_d, oT_out, QT=512, FT=256):
    nc = fw.nc
    NP = QT // 128
    with contextlib.ExitStack() as es:
        def sb(name, shape, dt=F32):
            return es.enter_context(nc.sbuf_tensor(name, shape, dt))
        W = sb("g_W", [128, NKC, 1536], BF16)
        Wab = sb("g_Wab", [128, NKC, 4], BF16)
        stage = [sb(f"g_st{i}", [128, 1536]) for i in range(2)]
        cm = sb("g_cm", [128, 15, 128])
        identb = sb("g_idb", [128, 128], BF16)
        onesb = sb("g_onesb", [128, 128], BF16)
        cw = sb("g_cw", [128, 2, 4, 4])
        gb = sb("g_gb", [128, NKC])
        hs = sb("g_hs", [128, 4])
        nexpA = sb("g_nexpA", [128, 2])
        og = sb("g_og", [128, 2])
        xt = [sb(f"g_x{i}", [128, NKC, FT]) for i in range(2)]
        sq = sb("g_sq", [128, NKC, FT], BF16)
        rstd = sb("g_rstd", [128, FT])
        xn = sb("g_xn", [128, NKC, FT], BF16)
        pre = [[sb(f"g_pre{h}{j}", [128, 3 + QT]) for j in range(4)] for h in range(2)]
        acc = [sb(f"g_acc{i}", [128, QT]) for i in range(2)]
        qks = [sb(f"g_qks{i}", [128, QT]) for i in range(2)]
        sqq = sb("g_sqq", [128, 2, QT], BF16)
        rn = sb("g_rn", [128, QT])
        qT = [sb(f"g_qT{h}", [128, QT], BF16) for h in range(2)]
        kT = [sb(f"g_kT{h}", [128, QT], BF16) for h in range(2)]
        vT = [sb(f"g_vT{h}", [128, 2, QT], BF16) for h in range(2)]
        zs = [sb(f"g_zs{h}", [128, 2, QT]) for h in range(2)]
        AB = sb("g_AB", [128, NP, 4])
        sc = sb("g_sc", [128, 24])
        GC = sb("g_GC", [128, 8])
        EX = sb("g_EX", [128, 8])
        G2 = [sb(f"g_G2{h}", [128, 128]) for h in range(2)]
        LB = [sb(f"g_LB{h}", [128, 128]) for h in range(2)]
        EE = [sb(f"g_EE{h}", [128, 512]) for h in range(2)]
        IV = [{n: sb(f"g_iv{n}{h}", [128, 128]) for n in ("Xf", "Yf", "Xa", "Ya", "Xb", "Yb", "Xc", "Yc", "Ma", "MTa",
                                                           "Xo", "Yo", "P1", "Q1")} for h in range(2)]
        Mf = [sb(f"g_M{h}", [128, 128]) for h in range(2)]
        qkT = [sb(f"g_qkT{h}", [128, 128], BF16) for h in range(2)]
        qdT = [sb(f"g_qdT{h}", [128, 128], BF16) for h in range(2)]
        vb = [sb(f"g_vb{h}", [128, 256]) for h in range(2)]
        kbg = [sb(f"g_kbg{h}", [128, 128]) for h in range(2)]
        kdz = [[sb(f"g_kdz{h}{c}", [128, 128], BF16) for c in range(2)] for h in range(2)]
        nkz = [[sb(f"g_nkz{h}{c}", [128, 128], BF16) for c in range(2)] for h in range(2)]
        wvs = [sb(f"g_wvs{h}", [128, 256]) for h in range(2)]
        vns = [sb(f"g_vns{h}", [128, 256], BF16) for h in range(2)]
        St = [sb(f"g_S{h}", [128, 256]) for h in range(2)]
        Sb = [sb(f"g_Sb{h}", [128, 256], BF16) for h in range(2)]
        ostg = [sb(f"g_ostg{h}", [128, 2, QT]) for h in range(2)]
        osq = sb("g_osq", [128, 2, QT], BF16)
        orn = sb("g_orn", [128, QT])
        otmp = sb("g_otmp", [128, QT])
        OS = [sb(f"g_OS{i}", [128, 2, QT], BF16) for i in range(2)]
        ps = [es.enter_context(nc.psum_tensor(f"g_ps{i}", [128, 512], F32)) for i in range(7)]
        tp = es.enter_context(nc.psum_tensor("g_tp", [128, 1024], BF16))

        B = {}

        def bf(name):
            if name not in B:
                B[name] = Buf(name, psum=name.endswith("bank") or name in ("fp0", "fp1", "tp"))
            return B[name]
        b_c = bf("consts")
        FP, bFP = ps[0:2], [bf("fp0"), bf("fp1")]
        EPs = ps[2]
        KQp = ps[3]
        INV = ps[4]
        WVp = ps[5]
        REC = ps[6]

        b_st = ring("st", 2)
        load_cast_bf16(fw, es, W, b_c, [w12_d[kc] for kc in range(NKC)], stage, b_st, 1536)
        fw.dma("sp", stage[0][:, 0:NKC * 4].rearrange("p (k c) -> p k c", c=4), wab_d.rearrange("k p c -> p k c"), writes=[b_st[0]])
        fw.op("dve", lambda v: v.tensor_copy(out=Wab[:], in_=stage[0][:, 0:NKC * 4].rearrange("p (k c) -> p k c", c=4)),
              reads=[b_st[0]], writes=[b_c])
        fw.dma("sp", cm[:], cm_d.rearrange("m p q -> p m q"), writes=[b_c])
        fw.dma("sp", cw[:], cw_d, writes=[b_c])
        fw.dma("sp", gb[:], gb_d, writes=[b_c])
        fw.dma("sp", hs[:], hs_d, writes=[b_c])
        fw.dma("sp", og[:], og_d, writes=[b_c])
        fw.op("dve", lambda v: v.tensor_copy(out=identb[:], in_=cm[:, CM_I, :]), reads=[b_c], writes=[b_c])
        fw.op("dve", lambda v: v.memset(onesb[:], 1.0), writes=[b_c])
        fw.op("act", lambda a: a.activation(out=nexpA[:], in_=hs[:, 0:2], func=AF.Exp), reads=[b_c], writes=[b_c])
        fw.op("dve", lambda v: v.tensor_scalar(out=nexpA[:], in0=nexpA[:], scalar1=-1.0, scalar2=None, op0=ALU.mult),
              reads=[b_c], writes=[b_c])
        for h in range(2):
            for j in range(4):
                fw.op("dve", lambda v, h=h, j=j: v.memset(pre[h][j][:, 0:3], 0.0), writes=[bf(f"pre{h}{j}")])
            fw.op("dve", lambda v, h=h: v.memset(St[h][:], 0.0), writes=[bf(f"S{h}")])
            fw.op("dve", lambda v, h=h: v.memset(Sb[h][:], 0.0), writes=[bf(f"Sb{h}")])
            for c in range(2):
                fw.op("dve", lambda v, h=h, c=c: v.memset(nkz[h][c][:], 0.0), writes=[bf(f"nkz{h}{c}")])

        if GDN_STOP[0] <= 1:
            fw.barrier()
            return
        def cmx(i):
            return cm[:, i, :]

        def dump(idx, ap, w, rb):
            if GDN_DBG[0] is None:
                return
            fw.dma("pool", GDN_DBG[0][idx, :, 0:w], ap, reads=[rb])

        nfp = [0]

        def fp_bank():
            k = nfp[0] % 2
            nfp[0] += 1
            return FP[k], bFP[k]

        nfe = 0
        nacc = 0
        for qi in range(S // QT):
            q0 = qi * QT
            for hf in range(QT // FT):
                t0 = q0 + hf * FT
                X, bX = xt[nfe % 2], bf(f"x{nfe % 2}")
                nfe += 1
                fw.dma("sp", X[:], xT[:, :, t0:t0 + FT].rearrange("k p t -> p k t"), writes=[bX])
                P, bP = fp_bank()
                rms_frontend(fw, X, bX, FT, sq, bf("sq"), onesb, b_c, P, bP, rstd, bf("rstd"))
                for kc in range(NKC):
                    fw.op("dve", lambda v, kc=kc: v.scalar_tensor_tensor(
                        out=xn[:, kc, :], in0=X[:, kc, :], scalar=gb[:, kc:kc + 1], in1=rstd[:, :FT],
                        op0=ALU.mult, op1=ALU.mult), reads=[bX, bf("rstd"), b_c], writes=[bf("xn")])
                for h in range(2):
                    for j in range(6):
                        P, bP = fp_bank()
                        col = (h * 6 + j) * 128
                        for kc in range(NKC):
                            fw.op("pe", lambda pe, kc=kc, P=P, col=col: pe.matmul(
                                P[:, :FT], lhsT=W[:, kc, col:col + 128], rhs=xn[:, kc, :],
                                start=(kc == 0), stop=(kc == NKC - 1)), reads=[b_c, bf("xn")], writes=[bP])
                        if j < 4:
                            fw.op("act", lambda a, P=P, h=h, j=j: a.activation(
                                out=pre[h][j][:, 3 + hf * FT:3 + (hf + 1) * FT], in_=P[:, :FT], func=AF.Copy),
                                reads=[bP], writes=[bf(f"pre{h}{j}")])
                        else:
                            fw.op("act", lambda a, P=P, h=h, j=j: a.activation(
                                out=zs[h][:, j - 4, hf * FT:(hf + 1) * FT], in_=P[:, :FT], func=AF.Silu),
                                reads=[bP], writes=[bf(f"zs{h}")])
                for tb in range(FT // 128):
                    P, bP = fp_bank()
                    for kc in range(NKC):
                        fw.op("pe", lambda pe, kc=kc, P=P, tb=tb: pe.matmul(
                            P[:, 0:4], lhsT=xn[:, kc, tb * 128:(tb + 1) * 128], rhs=Wab[:, kc, :],
                            start=(kc == 0), stop=(kc == NKC - 1)), reads=[b_c, bf("xn")], writes=[bP])
                    pidx = hf * (FT // 128) + tb
                    fw.op("act", lambda a, P=P, pidx=pidx: a.activation(out=AB[:, pidx, :], in_=P[:, 0:4], func=AF.Copy),
                          reads=[bP], writes=[bf("AB")])
            if GDN_STOP[0] <= 2:
                fw.barrier()
                return
            for h in range(2):
                for j in range(4):
                    A_, bA = acc[nacc % 2], bf(f"acc{nacc % 2}")
                    nacc += 1
                    pr, bpr = pre[h][j], bf(f"pre{h}{j}")
                    fw.op("dve", lambda v: v.tensor_scalar(out=A_[:], in0=pr[:, 3:3 + QT], scalar1=cw[:, h, j, 3:4],
                                                          scalar2=None, op0=ALU.mult), reads=[bpr, b_c], writes=[bA])
                    for i in (2, 1, 0):
                        fw.op("dve", lambda v, i=i: v.scalar_tensor_tensor(
                            out=A_[:], in0=pr[:, i:i + QT], scalar=cw[:, h, j, i:i + 1], in1=A_[:],
                            op0=ALU.mult, op1=ALU.add), reads=[bpr, b_c, bA], writes=[bA])
                    fw.op("act", lambda a: a.activation(out=pr[:, 0:3], in_=pr[:, QT:QT + 3], func=AF.Copy),
                          reads=[bpr], writes=[bpr])
                    if j >= 2:
                        fw.op("act", lambda a: a.activation(out=vT[h][:, j - 2, :], in_=A_[:], func=AF.Silu),
                              reads=[bA], writes=[bf(f"vT{h}")])
                    else:
                        Q_, bQ = qks[j], bf(f"qks{j}")
                        fw.op("act", lambda a: a.activation(out=Q_[:], in_=A_[:], func=AF.Silu), reads=[bA], writes=[bQ])
                        fw.op("act", lambda a: a.activation(out=sqq[:, j, :], in_=Q_[:], func=AF.Square),
                              reads=[bQ], writes=[bf("sqq")])
                        P, bP = fp_bank()
                        fw.op("pe", lambda pe: pe.matmul(P[:], lhsT=onesb[:], rhs=sqq[:, j, :], start=True, stop=True),
                              reads=[b_c, bf("sqq")], writes=[bP])
                        fw.op("dve", lambda v: v.tensor_scalar(out=rn[:], in0=P[:], scalar1=EPS, scalar2=None, op0=ALU.add),
                              reads=[bP], writes=[bf("rn")])
                        fw.op("act", lambda a: a.activation(out=rn[:], in_=rn[:], func=AF.Ln), reads=[bf("rn")], writes=[bf("rn")])
                        fw.op("act", lambda a: a.activation(out=rn[:], in_=rn[:], func=AF.Exp, scale=-0.5),
                              reads=[bf("rn")], writes=[bf("rn")])
                        dst, bdst = (qT[h], bf(f"qT{h}")) if j == 0 else (kT[h], bf(f"kT{h}"))
                        scl = (128.0 ** -0.5) if j == 0 else 1.0
                        fw.op("dve", lambda v: v.scalar_tensor_tensor(out=dst[:], in0=Q_[:], scalar=scl, in1=rn[:],
                                                                     op0=ALU.mult, op1=ALU.mult),
                              reads=[bQ, bf("rn")], writes=[bdst])
            if GDN_STOP[0] <= 3:
                fw.barrier()
                return
            for p in range(NP):
                ts_ = slice(p * 128, (p + 1) * 128)
                bsc = bf("sc")
                fw.op("dve", lambda v: v.tensor_tensor(out=sc[:, 0:2], in0=AB[:, p, 0:2], in1=hs[:, 2:4], op=ALU.add),
                      reads=[bf("AB"), b_c], writes=[bsc])
                fw.op("act", lambda a: a.activation(out=sc[:, 0:2], in_=sc[:, 0:2], func=AF.Exp), reads=[bsc], writes=[bsc])
                fw.op("act", lambda a: a.activation(out=sc[:, 0:2], in_=sc[:, 0:2], func=AF.Ln, bias=1.0), reads=[bsc], writes=[bsc])
                fw.op("dve", lambda v: v.tensor_tensor(out=sc[:, 2:4], in0=sc[:, 0:2], in1=nexpA[:], op=ALU.mult),
                      reads=[bsc, b_c], writes=[bsc])
                fw.op("act", lambda a: a.activation(out=sc[:, 4:6], in_=AB[:, p, 2:4], func=AF.Exp, scale=-1.0),
                      reads=[bf("AB"), bsc], writes=[bsc])
                fw.op("dve", lambda v: v.tensor_scalar(out=sc[:, 4:6], in0=sc[:, 4:6], scalar1=1.0, scalar2=None, op0=ALU.add),
                      reads=[bsc], writes=[bsc])
                fw.op("dve", lambda v: v.reciprocal(out=sc[:, 6:8], in_=sc[:, 4:6]), reads=[bsc], writes=[bsc])
                fw.op("act", lambda a: a.activation(out=sc[:, 8:10], in_=sc[:, 4:6], func=AF.Ln), reads=[bsc], writes=[bsc])
                bGp = bf("KQbank")
                for i, m in enumerate((CM_MLE, CM_ONESBD, CM_SEL0, CM_SEL1)):
                    fw.op("pe", lambda pe, i=i, m=m: pe.matmul(KQp[:, 256 + 2 * i:258 + 2 * i], lhsT=cmx(m), rhs=sc[:, 2:4],
                                                               start=True, stop=True), reads=[b_c, bsc], writes=[bGp])
                bGC, bEX = bf("GC"), bf("EX")
                fw.op("dve", lambda v: v.tensor_copy(out=GC[:], in_=KQp[:, 256:264]), reads=[bGp], writes=[bGC])
                fw.op("dve", lambda v: v.tensor_tensor(out=GC[:, 2:4], in0=GC[:, 2:4], in1=GC[:, 0:2], op=ALU.subtract),
                      reads=[bGC], writes=[bGC])
                fw.op("act", lambda a: a.activation(out=EX[:], in_=GC[:], func=AF.Exp), reads=[bGC], writes=[bEX])
                fw.op("dve", lambda v: v.tensor_tensor(out=sc[:, 10:12], in0=sc[:, 6:8], in1=EX[:, 0:2], op=ALU.mult),
                      reads=[bsc, bEX], writes=[bsc])
                fw.op("dve", lambda v: v.tensor_scalar(out=sc[:, 12:14], in0=EX[:, 2:4], scalar1=cm[:, CM_SEL0, 0:1],
                                                      scalar2=None, op0=ALU.mult), reads=[bEX, b_c, bsc], writes=[bsc])
                fw.op("dve", lambda v: v.tensor_scalar(out=sc[:, 14:16], in0=EX[:, 2:4], scalar1=cm[:, CM_SEL1, 0:1],
                                                      scalar2=None, op0=ALU.mult), reads=[bEX, b_c, bsc], writes=[bsc])
                if GDN_STOP[0] <= 4:
                    fw.barrier()
                    return
                for h in range(2):
                    bTP = bf("tp")
                    fw.op("pe", lambda pe: pe.transpose(tp[:, 0:128], kT[h][:, ts_], identb[:]),
                          reads=[bf(f"kT{h}"), b_c], writes=[bTP])
                    for c in range(2):
                        fw.op("pe", lambda pe, c=c: pe.transpose(tp[:, 128 + c * 128:256 + c * 128], vT[h][:, c, ts_], identb[:]),
                              reads=[bf(f"vT{h}"), b_c], writes=[bTP])
                    fw.op("dve", lambda v: v.tensor_scalar(out=vb[h][:], in0=tp[:, 128:384], scalar1=sc[:, 6 + h:7 + h],
                                                          scalar2=None, op0=ALU.mult), reads=[bTP, bsc], writes=[bf(f"vb{h}")])
                    fw.op("dve", lambda v: v.tensor_scalar(out=kbg[h][:], in0=tp[:, 0:128], scalar1=sc[:, 10 + h:11 + h],
                                                          scalar2=None, op0=ALU.mult), reads=[bTP, bsc], writes=[bf(f"kbg{h}")])
                    for c in range(2):
                        fw.op("dve", lambda v, c=c: v.tensor_scalar(
                            out=kdz[h][c][:], in0=tp[:, 0:128], scalar1=sc[:, 12 + 2 * c + h:13 + 2 * c + h],
                            scalar2=None, op0=ALU.mult), reads=[bTP, bsc], writes=[bf(f"kdz{h}{c}")])
                    if GDN_STOP[0] <= 5:
                        fw.barrier()
                        return
                    fw.op("dve", lambda v: v.tensor_scalar(out=G2[h][:], in0=cmx(CM_MLE), scalar1=sc[:, 2 + h:3 + h],
                                                          scalar2=None, op0=ALU.mult), reads=[b_c, bsc], writes=[bf(f"G2{h}")])
                    fw.op("dve", lambda v: v.tensor_scalar(out=LB[h][:], in0=cmx(CM_NEGID), scalar1=sc[:, 8 + h:9 + h],
                                                          scalar2=None, op0=ALU.mult), reads=[b_c, bsc], writes=[bf(f"LB{h}")])
                    bEP = bf("EPbank")
                    rdE = [b_c, bf(f"G2{h}"), bf(f"LB{h}")]
                    fw.op("pe", lambda pe: pe.matmul(EPs[:, 0:128], lhsT=cmx(CM_MGT), rhs=G2[h][:], start=True, stop=False),
                          reads=rdE, writes=[bEP])
                    fw.op("pe", lambda pe: pe.matmul(EPs[:, 0:128], lhsT=cmx(CM_ONESBD), rhs=LB[h][:], start=False, stop=False),
                          reads=rdE, writes=[bEP])
                    fw.op("pe", lambda pe: pe.matmul(EPs[:, 0:128], lhsT=cmx(CM_I), rhs=cmx(CM_NEGS), start=False, stop=True),
                          reads=rdE, writes=[bEP])
                    fw.op("pe", lambda pe: pe.matmul(EPs[:, 128:256], lhsT=cmx(CM_MGT), rhs=G2[h][:], start=True, stop=False),
                          reads=rdE, writes=[bEP])
                    fw.op("pe", lambda pe: pe.matmul(EPs[:, 128:256], lhsT=cmx(CM_I), rhs=cmx(CM_NEGI), start=False, stop=True),
                          reads=rdE, writes=[bEP])
                    fw.op("pe", lambda pe: pe.matmul(EPs[:, 256:384], lhsT=cmx(CM_ONES), rhs=G2[h][:], start=True, stop=True),
                          reads=rdE, writes=[bEP])
                    fw.op("pe", lambda pe: pe.matmul(EPs[:, 384:512], lhsT=G2[h][:], rhs=cmx(CM_MGT), start=True, stop=False),
                          reads=rdE, writes=[bEP])
                    fw.op("pe", lambda pe: pe.matmul(EPs[:, 384:512], lhsT=LB[h][:], rhs=cmx(CM_ONESBD), start=False, stop=False),
                          reads=rdE, writes=[bEP])
                    fw.op("pe", lambda pe: pe.matmul(EPs[:, 384:512], lhsT=cmx(CM_I), rhs=cmx(CM_NEGST), start=False, stop=True),
                          reads=rdE, writes=[bEP])
                    bEE = bf(f"EE{h}")
                    fw.op("act", lambda a: a.activation(out=EE[h][:], in_=EPs[:, 0:512], func=AF.Exp), reads=[bEP], writes=[bEE])
                    if GDN_STOP[0] <= 6:
                        fw.barrier()
                        return
                    bKQ = bf("KQbank")
                    fw.op("pe", lambda pe: pe.matmul(KQp[:, 0:128], lhsT=kT[h][:, ts_], rhs=kT[h][:, ts_], start=True, stop=True),
                          reads=[bf(f"kT{h}")], writes=[bKQ])
                    fw.op("pe", lambda pe: pe.matmul(KQp[:, 128:256], lhsT=kT[h][:, ts_], rhs=qT[h][:, ts_], start=True, stop=True),
                          reads=[bf(f"kT{h}"), bf(f"qT{h}")], writes=[bKQ])
                    T_ = IV[h]
                    bI = {n: bf(f"iv{n}{h}") for n in T_}
                    fw.op("dve", lambda v: v.tensor_tensor(out=T_["Xf"][:], in0=EE[h][:, 0:128], in1=KQp[:, 0:128], op=ALU.mult),
                          reads=[bEE, bKQ], writes=[bI["Xf"]])
                    fw.op("dve", lambda v: v.tensor_tensor(out=T_["Yf"][:], in0=EE[h][:, 384:512], in1=KQp[:, 0:128], op=ALU.mult),
                          reads=[bEE, bKQ], writes=[bI["Yf"]])
                    fw.op("dve", lambda v: v.tensor_tensor(out=qkT[h][:], in0=EE[h][:, 128:256], in1=KQp[:, 128:256], op=ALU.mult),
                          reads=[bEE, bKQ], writes=[bf(f"qkT{h}")])
                    fw.op("dve", lambda g: g.tensor_tensor(out=qdT[h][:], in0=qT[h][:, ts_], in1=EE[h][:, 256:384], op=ALU.mult),
                          reads=[bEE, bf(f"qT{h}")], writes=[bf(f"qdT{h}")])
                    if GDN_STOP[0] <= 8:
                        fw.barrier()
                        return
                    bINV = bf("INVbank")

                    def mm(col, l, r_):
                        fw.op("pe", lambda pe: pe.matmul(INV[:, col * 128:(col + 1) * 128], lhsT=T_[l][:], rhs=T_[r_][:],
                                                        start=True, stop=True), reads=[bI[l], bI[r_]], writes=[bINV])

                    def cp(dst, col):
                        fw.op("act", lambda a: a.activation(out=T_[dst][:], in_=INV[:, col * 128:(col + 1) * 128], func=AF.Copy),
                              reads=[bINV], writes=[bI[dst]])

                    def accp(dst, col, op):
                        fw.op("dve", lambda v: v.tensor_tensor(out=T_[dst][:], in0=T_[dst][:], in1=INV[:, col * 128:(col + 1) * 128], op=op),
                              reads=[bINV, bI[dst]], writes=[bI[dst]])

                    def msk(dst, src, m):
                        fw.op("dve", lambda g: g.tensor_tensor(out=T_[dst][:], in0=T_[src][:], in1=cmx(m), op=ALU.mult),
                              reads=[bI[src], b_c], writes=[bI[dst]])
                    msk("Xa", "Xf", CM_MK8)
                    msk("Ya", "Yf", CM_MK8)
                    for d_, s_ in (("Ma", "Xa"), ("MTa", "Ya")):
                        fw.op("dve", lambda v, d_=d_, s_=s_: v.scalar_tensor_tensor(
                            out=T_[d_][:], in0=T_[s_][:], scalar=-1.0, in1=cmx(CM_I), op0=ALU.mult, op1=ALU.add),
                            reads=[bI[s_], b_c], writes=[bI[d_]])
                    if GDN_STOP[0] == 8.1:
                        fw.barrier(); return
                    mm(0, "Ya", "Xa"); mm(1, "Xa", "Ya")
                    if GDN_STOP[0] == 8.2:
                        fw.barrier(); return
                    cp("Xb", 0); cp("Yb", 1)
                    if GDN_STOP[0] == 8.3:
                        fw.barrier(); return
                    mm(0, "Yb", "Xb"); mm(1, "Yb", "Ma"); mm(2, "Xb", "Yb"); mm(3, "Xb", "MTa")
                    cp("Xc", 0); cp("Yc", 2); accp("Ma", 1, ALU.add); accp("MTa", 3, ALU.add)
                    if GDN_STOP[0] == 8.4:
                        fw.barrier(); return
                    mm(0, "Yc", "Ma"); mm(1, "Xc", "MTa")
                    accp("Ma", 0, ALU.add); accp("MTa", 1, ALU.add)
                    if GDN_STOP[0] == 8.5:
                        fw.barrier(); return
                    for mk in (CM_MK16, CM_MK32, CM_MK64):
                        last = (mk == CM_MK64)
                        msk("Xo", "Xf", mk)
                        msk("Yo", "Yf", mk)
                        mm(0, "Yo", "Ma")
                        if not last:
                            mm(1, "Xo", "MTa")
                        cp("P1", 0)
                        if not last:
                            cp("Q1", 1)
                        mm(2, "MTa", "P1")
                        if not last:
                            mm(3, "Ma", "Q1")
                            accp("Ma", 2, ALU.subtract)
                            accp("MTa", 3, ALU.subtract)
                        else:
                            fw.op("dve", lambda v: v.tensor_tensor(out=Mf[h][:], in0=T_["Ma"][:], in1=INV[:, 256:384], op=ALU.subtract),
                                  reads=[bINV, bI["Ma"]], writes=[bf(f"M{h}")])
                    if GDN_STOP[0] <= 9:
                        fw.barrier()
                        return
                    bWV, bKC = bf("WVbank"), bf("WVbank")
                    for hf2 in range(2):
                        fw.op("pe", lambda pe, hf2=hf2: pe.matmul(WVp[:, hf2 * 128:(hf2 + 1) * 128], lhsT=Mf[h][:],
                                                                  rhs=vb[h][:, hf2 * 128:(hf2 + 1) * 128], start=True, stop=True),
                              reads=[bf(f"M{h}"), bf(f"vb{h}")], writes=[bWV])
                    fw.op("pe", lambda pe: pe.matmul(WVp[:, 256:384], lhsT=kbg[h][:], rhs=Mf[h][:], start=True, stop=True),
                          reads=[bf(f"M{h}"), bf(f"kbg{h}")], writes=[bKC])
                    fw.op("act", lambda a: a.activation(out=wvs[h][:], in_=WVp[:, 0:256], func=AF.Copy), reads=[bWV], writes=[bf(f"wvs{h}")])
                    for c in range(2):
                        fw.op("act", lambda a, c=c: a.activation(out=nkz[h][c][:, c * 64:(c + 1) * 64],
                                                                 in_=WVp[:, 256 + c * 64:256 + (c + 1) * 64], func=AF.Copy, scale=-1.0),
                              reads=[bKC], writes=[bf(f"nkz{h}{c}")])
                if GDN_STOP[0] <= 10:
                    fw.barrier()
                    return
                if qi == 0 and p == 0:
                    dump(0, sc[:, 0:16], 16, bsc)
                    dump(1, GC[:, :], 8, bGC)
                    dump(2, EX[:, :], 8, bEX)
                    dump(3, EE[0][:, :], 512, bf("EE0"))
                    dump(4, Mf[0][:, :], 128, bf("M0"))
                    dump(5, vb[0][:, :], 256, bf("vb0"))
                    dump(6, kbg[0][:, :], 128, bf("kbg0"))
                    dump(7, wvs[0][:, :], 256, bf("wvs0"))
                    dump(8, G2[0][:, :], 128, bf("G20"))
                    dump(9, LB[0][:, :], 128, bf("LB0"))
                    dump(10, AB[:, 0, :], 4, bf("AB"))
                    dump(11, IV[0]["Xf"][:, :], 128, bf("ivXf0"))
                    dump(12, IV[0]["Yf"][:, :], 128, bf("ivYf0"))
                if 1 in GDN_BAR[0]:
                    fw.barrier()
                for c in range(2):
                    for h in range(2):
                        if 2 in GDN_BAR[0]:
                            fw.barrier()
                        bS, bSb, bvn = bf(f"S{h}"), bf(f"Sb{h}"), bf(f"vns{h}")
                        bVN, bSU, bOT = bf("RECbank"), bf("RECbank"), bf("WVbank")
                        fw.op("pe", lambda pe: pe.matmul(REC[:, 0:256], lhsT=nkz[h][c][:], rhs=Sb[h][:], start=True, stop=True),
                              reads=[bf(f"nkz{h}{c}"), bSb], writes=[bVN])
                        fw.op("dve", lambda v: v.tensor_tensor(out=vns[h][:], in0=wvs[h][:], in1=REC[:, 0:256], op=ALU.add),
                              reads=[bf(f"wvs{h}"), bVN], writes=[bvn])
                        cs = slice(c * 64, (c + 1) * 64)
                        for dvc in range(2):
                            ds_ = slice(dvc * 128, (dvc + 1) * 128)
                            oc = slice(384 + dvc * 64, 384 + (dvc + 1) * 64)
                            fw.op("pe", lambda pe: pe.matmul(WVp[:, oc], lhsT=Sb[h][:, ds_], rhs=qdT[h][:, cs], start=True, stop=False,
                                                            skip_group_check=True), reads=[bSb, bf(f"qdT{h}")], writes=[bOT])
                            fw.op("pe", lambda pe: pe.matmul(WVp[:, oc], lhsT=vns[h][:, ds_], rhs=qkT[h][:, cs], start=False, stop=True,
                                                            skip_group_check=True), reads=[bvn, bf(f"qkT{h}")], writes=[bOT])
                        fw.op("pe", lambda pe: pe.matmul(REC[:, 256:512], lhsT=kdz[h][c][:], rhs=vns[h][:], start=True, stop=True),
                              reads=[bf(f"kdz{h}{c}"), bvn], writes=[bSU])
                        fw.op("dve", lambda v: v.scalar_tensor_tensor(out=St[h][:], in0=St[h][:], scalar=EX[:, 4 + 2 * c + h:5 + 2 * c + h],
                                                                     in1=REC[:, 256:512], op0=ALU.mult, op1=ALU.add),
                              reads=[bS, bSU, bf("EX")], writes=[bS])
                        fw.op("act", lambda a: a.activation(out=Sb[h][:], in_=St[h][:], func=AF.Copy), reads=[bS], writes=[bSb])
                        off = p * 128 + c * 64
                        fw.op("act", lambda a: a.activation(out=ostg[h][:, :, off:off + 64],
                                                            in_=WVp[:, 384:512].rearrange("p (d t) -> p d t", d=2), func=AF.Copy),
                              reads=[bOT], writes=[bf(f"ostg{h}")])
                if qi == 0 and p == 0:
                    dump(13, St[0][:, :], 256, bf("S0"))
                    dump(14, St[1][:, :], 256, bf("S1"))
                    dump(15, ostg[0][:, 0, 0:128], 128, bf("ostg0"))
            if GDN_STOP[0] <= 11:
                fw.barrier()
                return
            for h in range(2):
                bo = bf(f"ostg{h}")
                for dvc in range(2):
                    fw.op("act", lambda a, dvc=dvc: a.activation(out=osq[:, dvc, :], in_=ostg[h][:, dvc, :], func=AF.Square),
                          reads=[bo], writes=[bf("osq")])
                P, bP = fp_bank()
                for dvc in range(2):
                    fw.op("pe", lambda pe, dvc=dvc: pe.matmul(P[:], lhsT=onesb[:], rhs=osq[:, dvc, :], start=(dvc == 0), stop=(dvc == 1)),
                          reads=[b_c, bf("osq")], writes=[bP])
                fw.op("dve", lambda v: v.tensor_scalar(out=orn[:], in0=P[:], scalar1=1.0 / 256.0, scalar2=EPS, op0=ALU.mult, op1=ALU.add),
                      reads=[bP], writes=[bf("orn")])
                fw.op("act", lambda a: a.activation(out=orn[:], in_=orn[:], func=AF.Ln), reads=[bf("orn")], writes=[bf("orn")])
                fw.op("act", lambda a: a.activation(out=orn[:], in_=orn[:], func=AF.Exp, scale=-0.5), reads=[bf("orn")], writes=[bf("orn")])
                k = (qi * 2 + h) % 2
                bOS = bf(f"OS{k}")
                for dvc in range(2):
                    fw.op("dve", lambda v, dvc=dvc: v.scalar_tensor_tensor(out=otmp[:], in0=ostg[h][:, dvc, :], scalar=og[:, dvc:dvc + 1],
                                                                          in1=orn[:], op0=ALU.mult, op1=ALU.mult),
                          reads=[bo, bf("orn"), b_c], writes=[bf("otmp")])
                    fw.op("dve", lambda v, dvc=dvc: v.tensor_tensor(out=OS[k][:, dvc, :], in0=otmp[:], in1=zs[h][:, dvc, :], op=ALU.mult),
                          reads=[bf("otmp"), bf(f"zs{h}")], writes=[bOS])
                fw.dma("pool", oT_out[2 * h:2 * h + 2, :, q0:q0 + QT].rearrange("c p t -> p c t"), OS[k][:], reads=[bOS])
        fw.barrier()


def build_gdn_program(S, has_prev):
    nc = bass.Bass("TRN2", target_bir_lowering=False)
    fw = FW(nc)
    xT_in = dram_in(nc, "xT_in", [NKC, 128, S])
    w12 = dram_in(nc, "w12", [NKC, 128, 1536])
    wab = dram_in(nc, "wab", [NKC, 128, 4])
    cw = dram_in(nc, "cw", [128, 2, 4, 4])
    gb = dram_in(nc, "gb", [128, NKC])
    hs = dram_in(nc, "hs", [128, 4])
    og = dram_in(nc, "og", [128, 2])
    cmd = dram_in(nc, "cm", [15, 128, 128])
    oT_out = dram_out(nc, "oT_out", [4, 128, S], BF16)
    if GDN_DBG[0] is not None:
        GDN_DBG[0] = dram_out(nc, "dbg", [16, 128, 512])
    if has_prev:
        oT_prev = dram_in(nc, "oT_prev", [16, 128, S], BF16)
        w_out = dram_in(nc, "w_out", [16, 128, D])
        xT_out = dram_out(nc, "xT_out", [NKC, 128, S])
        prologue_phase(fw, S, 16, xT_in, oT_prev, w_out, xT_out)
        xsrc = xT_out
    else:
        xsrc = xT_in
    gdn_layer_phase(fw, S, xsrc, w12, wab, cw, gb, hs, og, cmd, oT_out)
    fw.close()
    return nc


def gdn_core_inputs(w_in, conv, a_log, dt_bias, out_g, norm_g, hg):
    cols = []
    cwm = np.zeros((128, 2, 4, 4), np.float32)
    for hh in range(2):
        h = 2 * hg + hh
        qc = np.arange(h * 128, (h + 1) * 128)
        kc_ = 1024 + np.arange(h * 128, (h + 1) * 128)
        vc = 2048 + np.arange(h * 256, (h + 1) * 256)
        zc = 4096 + np.arange(h * 256, (h + 1) * 256)
        cols += [qc, kc_, vc[:128], vc[128:], zc[:128], zc[128:]]
        for j, cc in enumerate([qc, kc_, vc[:128], vc[128:]]):
            cwm[:, hh, j, :] = conv[:, cc].T
    cols = np.concatenate(cols)
    w12 = rows(np.ascontiguousarray(w_in[:, cols]))
    hsel = [2 * hg, 2 * hg + 1]
    wab = rows(np.ascontiguousarray(w_in[:, [6144 + hsel[0], 6144 + hsel[1], 6152 + hsel[0], 6152 + hsel[1]]]))
    hsv = np.concatenate([a_log[hsel], dt_bias[hsel]]).astype(np.float32)
    return {"w12": w12, "wab": wab, "cw": cwm, "gb": gain_pk(norm_g),
            "hs": np.ascontiguousarray(np.broadcast_to(hsv[None, :], (128, 4))),
            "og": np.ascontiguousarray(out_g.reshape(2, 128).T), "cm": gdn_consts()}


def build_final_program(Sl, prev_kc):
    nc = bass.Bass("TRN2", target_bir_lowering=False)
    fw = FW(nc)
    xT_in = dram_in(nc, "xT_in", [NKC, 128, Sl])
    oT_prev = dram_in(nc, "oT_prev", [prev_kc, 128, Sl], BF16)
    w_out = dram_in(nc, "w_out", [prev_kc, 128, D])
    gfin = dram_in(nc, "gfin", [128, NKC])
    outT = dram_out(nc, "outT", [NKC, 128, Sl])
    prologue_phase(fw, Sl, prev_kc, xT_in, oT_prev, w_out, outT, final_g=gfin)
    fw.close()
    return nc


N_CORES = 8


def _run(nc, maps):
    return run_bass_kernel_spmd(nc, maps, core_ids=list(range(N_CORES))).results


def kernel(x, a_norm, a_w_in, a_conv, a_A_log, a_dt_bias, a_out_norm, a_w_out,
           kv_norm, w_kv, b_norm, b_w_in, b_w_out, final_norm):
    f = lambda a: np.asarray(a, dtype=np.float32)
    x, a_norm, a_w_in, a_conv, a_A_log, a_dt_bias, a_out_norm, a_w_out = map(f, (
        x, a_norm, a_w_in, a_conv, a_A_log, a_dt_bias, a_out_norm, a_w_out))
    kv_norm, w_kv, b_norm, b_w_in, b_w_out, final_norm = map(f, (kv_norm, w_kv, b_norm, b_w_in, b_w_out, final_norm))
    B, S, _ = x.shape
    xT = [fm(x[b]) for b in range(B)]

    def gather(res, key):
        return [np.concatenate([np.asarray(res[4 * b + g][key]) for g in range(4)], axis=0) for b in range(B)]

    nc = build_gdn_program(S, False)
    maps = []
    for c in range(N_CORES):
        b, hg = c // 4, c % 4
        m = gdn_core_inputs(a_w_in[0], a_conv[0], a_A_log[0], a_dt_bias[0], a_out_norm[0], a_norm[0], hg)
        m["xT_in"] = xT[b]
        maps.append(m)
    res = _run(nc, maps)
    oT = gather(res, "oT_out")
    nc = build_gdn_program(S, True)
    maps = []
    for c in range(N_CORES):
        b, hg = c // 4, c % 4
        m = gdn_core_inputs(a_w_in[1], a_conv[1], a_A_log[1], a_dt_bias[1], a_out_norm[1], a_norm[1], hg)
        m["xT_in"] = xT[b]
        m["oT_prev"] = oT[b]
        m["w_out"] = rows(a_w_out[0])
        maps.append(m)
    res = _run(nc, maps)
    xT = [np.asarray(res[4 * b]["xT_out"]) for b in range(B)]
    oT = gather(res, "oT_out")
    ctri, cmask = sb_consts()
    nc = build_sb_program(S, True, 16)
    maps = []
    for c in range(N_CORES):
        b, h = c // 4, c % 4
        hs_ = slice(h * 256, (h + 1) * 256)
        maps.append({"xT_in": xT[b], "oT_prev": oT[b], "w_out": rows(a_w_out[1]),
                     "wq": rows(b_w_in[0][:, hs_]), "wz": rows(b_w_in[0][:, 1024:][:, hs_]),
                     "wk": rows(w_kv[:, hs_]), "wv": rows(w_kv[:, 1024:][:, hs_]),
                     "gb": gain_pk(b_norm[0]), "gkv": gain_pk(kv_norm), "ctri": ctri, "cmask": cmask})
    res = _run(nc, maps)
    xT = [np.asarray(res[4 * b]["xT_out"]) for b in range(B)]
    oT = gather(res, "oT_out")
    KT = [np.asarray(res[c]["KT"]) for c in range(N_CORES)]
    V = [np.asarray(res[c]["V"]) for c in range(N_CORES)]
    nc = build_sb_program(S, False, 8)
    maps = []
    for c in range(N_CORES):
        b, h = c // 4, c % 4
        hs_ = slice(h * 256, (h + 1) * 256)
        maps.append({"xT_in": xT[b], "oT_prev": oT[b], "w_out": rows(b_w_out[0]),
                     "wq": rows(b_w_in[1][:, hs_]), "wz": rows(b_w_in[1][:, 1024:][:, hs_]),
                     "gb": gain_pk(b_norm[1]), "ctri": ctri, "cmask": cmask, "KT": KT[c], "V": V[c]})
    res = _run(nc, maps)
    xT = [np.asarray(res[4 * b]["xT_out"]) for b in range(B)]
    oT = gather(res, "oT_out")
    Sl = S // 4
    nc = build_final_program(Sl, 8)
    maps = []
    for c in range(N_CORES):
        b, q = c // 4, c % 4
        ts_ = slice(q * Sl, (q + 1) * Sl)
        maps.append({"xT_in": np.ascontiguousarray(xT[b][:, :, ts_]), "oT_prev": np.ascontiguousarray(oT[b][:, :, ts_]),
                     "w_out": rows(b_w_out[1]), "gfin": gain_pk(final_norm)})
    res = _run(nc, maps)
    out = np.empty((B, S, D), np.float32)
    for c in range(N_CORES):
        b, q = c // 4, c % 4
        o = np.asarray(res[c]["outT"])
        out[b, q * Sl:(q + 1) * Sl, :] = o.reshape(D, Sl).T
    return out
```
